# Optimizing a Trainium2 kernel written in Bass

```python
import numpy as np
import jax, jax.numpy as jnp
from jax import lax

D_MODEL = 1024
BATCH = 4
SEQ = 4096
DEPTH = 2

RET_HEADS = 4
RET_DV = D_MODEL // RET_HEADS
RET_DK = RET_DV // 2
RET_CHUNK = 128
SC_WIDTH = D_MODEL
CONV_WIDTH = 3
NSA_DH = 64
NSA_HEADS = D_MODEL // NSA_DH
NSA_KV_HEADS = 4
NSA_HPG = NSA_HEADS // NSA_KV_HEADS
CMP_BLOCK = 32
CMP_STRIDE = 16
CMP_HIDDEN = 256
SLC_BLOCK = 64
N_SELECT = 16
WINDOW = 512
Q_CHUNK = 64
D_FF = ((8 * D_MODEL // 3 + 127) // 128) * 128
N_BRANCH = 3
EPS = 1e-6
NEG = -1e30
FORCE = 1e6

RET_QK_W = RET_HEADS * RET_DK
RET_V_W = RET_HEADS * RET_DV
NSA_Q_W = NSA_HEADS * NSA_DH
NSA_KV_W = NSA_KV_HEADS * NSA_DH
IN_SPLITS = (RET_QK_W, RET_QK_W, RET_V_W, RET_V_W, SC_WIDTH, SC_WIDTH, SC_WIDTH,
             NSA_Q_W, 6 * NSA_KV_W, 3 * NSA_HEADS, N_BRANCH * D_MODEL)
IN_WIDTH = sum(IN_SPLITS)
IN_OFFSETS = tuple(np.cumsum(IN_SPLITS)[:-1].tolist())

kernel_name = 'hybrid_retention_shortconv_nsa_block'


def rms_norm(x, w):
    xf = x.astype(jnp.float32)
    y = xf * lax.rsqrt(jnp.mean(xf * xf, axis=-1, keepdims=True) + EPS)
    return (y * w.astype(jnp.float32)).astype(x.dtype)


def causal_dwconv(x, w):
    c = x.shape[-1]
    return lax.conv_general_dilated(x, w[:, None, :].astype(x.dtype), window_strides=(1,),
                                    padding=((CONV_WIDTH - 1, 0),),
                                    dimension_numbers=('NWC', 'WIO', 'NWC'),
                                    feature_group_count=c)


def rotary(x, cos, sin):
    x1, x2 = jnp.split(x, 2, axis=-1)
    return jnp.concatenate([x1 * cos - x2 * sin, x1 * sin + x2 * cos], axis=-1)


def retention(q, k, v, g, norm_w):
    b, s = q.shape[:2]
    dt = q.dtype
    pos = jnp.arange(s, dtype=jnp.float32)
    theta = 10000.0 ** (-jnp.linspace(0.0, 1.0, RET_DK // 2, dtype=jnp.float32))
    ang = pos[:, None] * theta[None, :]
    cos = jnp.cos(ang)[None, :, None, :].astype(dt)
    sin = jnp.sin(ang)[None, :, None, :].astype(dt)
    q = rotary(q, cos, sin)
    k = rotary(k, cos, sin) * (RET_DK ** -0.5)
    log_gamma = jnp.log1p(-(2.0 ** (-5.0 - jnp.arange(RET_HEADS, dtype=jnp.float32))))
    c = RET_CHUNK
    n = s // c
    qc = q.reshape(b, n, c, RET_HEADS, RET_DK)
    kc = k.reshape(b, n, c, RET_HEADS, RET_DK)
    vc = v.reshape(b, n, c, RET_HEADS, RET_DV)
    j = jnp.arange(c, dtype=jnp.float32)
    rel = j[:, None] - j[None, :]
    dmask = jnp.where(rel >= 0, jnp.exp(log_gamma[:, None, None] * jnp.maximum(rel, 0.0)), 0.0).astype(dt)
    scores = jnp.einsum('bnqhd,bnkhd->bnhqk', qc, kc) * dmask
    o_intra = jnp.einsum('bnhqk,bnkhv->bnqhv', scores, vc)
    zeta = jnp.exp(log_gamma[:, None] * (c - 1 - j)[None, :]).astype(dt)
    kv = jnp.einsum('bnchd,bnchv,hc->nbhdv', kc, vc, zeta)
    chunk_decay = jnp.exp(log_gamma * c).astype(dt)[None, :, None, None]

    def step(state, kv_i):
        return state * chunk_decay + kv_i, state

    _, prev = lax.scan(step, jnp.zeros((b, RET_HEADS, RET_DK, RET_DV), dt), kv)
    xi = jnp.exp(log_gamma[:, None] * (j + 1.0)[None, :]).astype(dt)
    o_cross = jnp.einsum('bnchd,nbhdv,hc->bnchv', qc, prev, xi)
    o = (o_intra + o_cross).reshape(b, s, RET_HEADS, RET_DV).astype(jnp.float32)
    mu = jnp.mean(o, axis=-1, keepdims=True)
    var = jnp.mean(jnp.square(o - mu), axis=-1, keepdims=True)
    o = ((o - mu) * lax.rsqrt(var + EPS)).reshape(b, s, RET_V_W) * norm_w.astype(jnp.float32)
    return (jax.nn.silu(g.astype(jnp.float32)) * o).astype(dt)


def compress(kv, pos, w1, w2):
    b, s = kv.shape[:2]
    nc = (s - CMP_BLOCK) // CMP_STRIDE + 1
    idx = np.arange(nc)[:, None] * CMP_STRIDE + np.arange(CMP_BLOCK)[None, :]
    blk = kv[:, idx] + pos[None, None, :, None, :]
    blk = blk.transpose(0, 1, 3, 2, 4).reshape(b, nc, NSA_KV_HEADS, CMP_BLOCK * NSA_DH)
    return jax.nn.gelu(blk @ w1) @ w2


def nsa_attention(q, kc_raw, vc_raw, ks, vs, kw, vw, gates, cmp_pos, cmp_w1, cmp_w2):
    b, s = q.shape[:2]
    dt = q.dtype
    scale = NSA_DH ** -0.5
    kc = compress(kc_raw, cmp_pos[0], cmp_w1[0], cmp_w2[0])
    vc = compress(vc_raw, cmp_pos[1], cmp_w1[1], cmp_w2[1])
    nc = kc.shape[1]
    ns = s // SLC_BLOCK
    n_sel = min(N_SELECT, ns)
    ci = np.arange(nc) * CMP_STRIDE
    sj = np.arange(ns) * SLC_BLOCK
    overlap = jnp.asarray(((ci[:, None] < sj[None, :] + SLC_BLOCK) &
                           (ci[:, None] + CMP_BLOCK > sj[None, :])).astype(np.float32))
    cmp_end = jnp.asarray(ci + CMP_BLOCK - 1)
    ks_blk = ks.reshape(b, ns, SLC_BLOCK, NSA_KV_HEADS, NSA_DH).transpose(0, 3, 1, 2, 4)
    vs_blk = vs.reshape(b, ns, SLC_BLOCK, NSA_KV_HEADS, NSA_DH).transpose(0, 3, 1, 2, 4)
    kw_pad = jnp.pad(kw, ((0, 0), (WINDOW, 0), (0, 0), (0, 0)))
    vw_pad = jnp.pad(vw, ((0, 0), (WINDOW, 0), (0, 0), (0, 0)))
    qg = q.reshape(b, s, NSA_KV_HEADS, NSA_HPG, NSA_DH)
    bi = jnp.arange(b)[:, None, None, None]
    gi = jnp.arange(NSA_KV_HEADS)[None, :, None, None]
    blk_ids = jnp.arange(ns)

    def chunk(cidx):
        t0 = cidx * Q_CHUNK
        t = t0 + jnp.arange(Q_CHUNK)
        qc = lax.dynamic_slice_in_dim(qg, t0, Q_CHUNK, axis=1)
        s1 = jnp.einsum('bqghd,bcgd->bghqc', qc, kc).astype(jnp.float32) * scale
        valid1 = cmp_end[None, :] <= t[:, None]
        any1 = (t >= CMP_BLOCK - 1).astype(jnp.float32)[:, None]
        p1 = jax.nn.softmax(jnp.where(valid1, s1, NEG), axis=-1) * any1
        o_cmp = jnp.einsum('bghqc,bcgd->bqghd', p1.astype(dt), vc)
        imp = jnp.einsum('bghqc,cs->bgqs', p1, overlap)
        cur = t // SLC_BLOCK
        causal_blk = blk_ids[None, :] * SLC_BLOCK <= t[:, None]
        forced = (blk_ids[None, :] == 0) | (blk_ids[None, :] == cur[:, None]) | (blk_ids[None, :] == cur[:, None] - 1)
        imp = jnp.where(forced & causal_blk, FORCE, imp)
        imp = jnp.where(causal_blk, imp, NEG)
        _, idx = lax.top_k(imp, n_sel)
        kg = ks_blk[bi, gi, idx]
        vg = vs_blk[bi, gi, idx]
        s2 = jnp.einsum('bqghd,bgqnld->bghqnl', qc, kg).astype(jnp.float32) * scale
        kpos2 = idx[..., None] * SLC_BLOCK + jnp.arange(SLC_BLOCK)
        mask2 = kpos2 <= t[None, None, :, None, None]
        s2 = jnp.where(mask2[:, :, None], s2, NEG)
        p2 = jax.nn.softmax(s2.reshape(s2.shape[:4] + (-1,)), axis=-1).reshape(s2.shape)
        o_slc = jnp.einsum('bghqnl,bgqnld->bqghd', p2.astype(dt), vg)
        kwc = lax.dynamic_slice_in_dim(kw_pad, t0, Q_CHUNK + WINDOW, axis=1)
        vwc = lax.dynamic_slice_in_dim(vw_pad, t0, Q_CHUNK + WINDOW, axis=1)
        kpos3 = t0 - WINDOW + jnp.arange(Q_CHUNK + WINDOW)
        mask3 = (kpos3[None, :] <= t[:, None]) & (kpos3[None, :] > t[:, None] - WINDOW) & (kpos3[None, :] >= 0)
        s3 = jnp.einsum('bqghd,brgd->bghqr', qc, kwc).astype(jnp.float32) * scale
        p3 = jax.nn.softmax(jnp.where(mask3, s3, NEG), axis=-1)
        o_win = jnp.einsum('bghqr,brgd->bqghd', p3.astype(dt), vwc)
        return o_cmp, o_slc, o_win

    o_cmp, o_slc, o_win = lax.map(chunk, jnp.arange(s // Q_CHUNK))

    def unchunk(o):
        return o.transpose(1, 0, 2, 3, 4, 5).reshape(b, s, NSA_HEADS, NSA_DH)

    o = (gates[..., 0:1] * unchunk(o_cmp) + gates[..., 1:2] * unchunk(o_slc)
         + gates[..., 2:3] * unchunk(o_win))
    return o.reshape(b, s, NSA_Q_W)


def token_mixer(h, w_in, ret_norm_w, w_ret_out, sc_conv_w, w_sc_out,
                cmp_pos, cmp_w1, cmp_w2, w_nsa_out, w_mix_out):
    b, s, _ = h.shape
    z = h @ w_in
    rq, rk, rv, rg, sb, sc, sx, nq, nkv, ng, mg = jnp.split(z, IN_OFFSETS, axis=-1)
    y_ret = retention(rq.reshape(b, s, RET_HEADS, RET_DK), rk.reshape(b, s, RET_HEADS, RET_DK),
                      rv.reshape(b, s, RET_HEADS, RET_DV), rg, ret_norm_w)
    y_sc = sb * causal_dwconv(sc * sx, sc_conv_w)
    k_c, v_c, k_s, v_s, k_w, v_w = jnp.split(nkv.reshape(b, s, 6 * NSA_KV_HEADS, NSA_DH), 6, axis=2)
    y_nsa = nsa_attention(nq.reshape(b, s, NSA_HEADS, NSA_DH), k_c, v_c, k_s, v_s, k_w, v_w,
                          jax.nn.sigmoid(ng).reshape(b, s, NSA_HEADS, 3), cmp_pos, cmp_w1, cmp_w2)
    g_ret, g_sc, g_nsa = jnp.split(jax.nn.sigmoid(mg), N_BRANCH, axis=-1)
    merged = g_ret * (y_ret @ w_ret_out) + g_sc * (y_sc @ w_sc_out) + g_nsa * (y_nsa @ w_nsa_out)
    return merged @ w_mix_out


def conv_glu_ffn(h, w_up, conv_w, w_down):
    u = causal_dwconv(h @ w_up, conv_w)
    a, v = jnp.split(u, 2, axis=-1)
    return (jax.nn.silu(a) * v) @ w_down


def setup_inputs(seed: int = 0) -> dict:
    key = jax.random.key(seed)
    ks = jax.random.split(key, 17)
    nrm = jax.random.normal
    f32 = jnp.float32
    return {
        'x': nrm(ks[0], (BATCH, SEQ, D_MODEL), f32),
        'attn_norm_w': 1.0 + 0.02 * nrm(ks[1], (DEPTH, D_MODEL), f32),
        'w_in': nrm(ks[2], (DEPTH, D_MODEL, IN_WIDTH), f32) * D_MODEL ** -0.5,
        'ret_norm_w': 1.0 + 0.02 * nrm(ks[3], (DEPTH, RET_V_W), f32),
        'w_ret_out': nrm(ks[4], (DEPTH, RET_V_W, D_MODEL), f32) * RET_V_W ** -0.5,
        'sc_conv_w': nrm(ks[5], (DEPTH, CONV_WIDTH, SC_WIDTH), f32) * CONV_WIDTH ** -0.5,
        'w_sc_out': nrm(ks[6], (DEPTH, SC_WIDTH, D_MODEL), f32) * SC_WIDTH ** -0.5,
        'nsa_cmp_pos': 0.02 * nrm(ks[7], (DEPTH, 2, CMP_BLOCK, NSA_DH), f32),
        'nsa_cmp_w1': nrm(ks[8], (DEPTH, 2, CMP_BLOCK * NSA_DH, CMP_HIDDEN), f32) * (CMP_BLOCK * NSA_DH) ** -0.5,
        'nsa_cmp_w2': nrm(ks[9], (DEPTH, 2, CMP_HIDDEN, NSA_DH), f32) * CMP_HIDDEN ** -0.5,
        'w_nsa_out': nrm(ks[10], (DEPTH, NSA_Q_W, D_MODEL), f32) * NSA_Q_W ** -0.5,
        'w_mix_out': nrm(ks[11], (DEPTH, D_MODEL, D_MODEL), f32) * D_MODEL ** -0.5,
        'ffn_norm_w': 1.0 + 0.02 * nrm(ks[12], (DEPTH, D_MODEL), f32),
        'w_ffn_up': nrm(ks[13], (DEPTH, D_MODEL, 2 * D_FF), f32) * D_MODEL ** -0.5,
        'ffn_conv_w': nrm(ks[14], (DEPTH, CONV_WIDTH, 2 * D_FF), f32) * CONV_WIDTH ** -0.5,
        'w_ffn_down': nrm(ks[15], (DEPTH, D_FF, D_MODEL), f32) * D_FF ** -0.5,
        'final_norm_w': 1.0 + 0.02 * nrm(ks[16], (D_MODEL,), f32),
    }


def reference(x, attn_norm_w, w_in, ret_norm_w, w_ret_out, sc_conv_w, w_sc_out,
              nsa_cmp_pos, nsa_cmp_w1, nsa_cmp_w2, w_nsa_out, w_mix_out,
              ffn_norm_w, w_ffn_up, ffn_conv_w, w_ffn_down, final_norm_w):
    for l in range(DEPTH):
        h = rms_norm(x, attn_norm_w[l])
        x = x + token_mixer(h, w_in[l], ret_norm_w[l], w_ret_out[l], sc_conv_w[l], w_sc_out[l],
                            nsa_cmp_pos[l], nsa_cmp_w1[l], nsa_cmp_w2[l], w_nsa_out[l], w_mix_out[l])
        h = rms_norm(x, ffn_norm_w[l])
        x = x + conv_glu_ffn(h, w_ffn_up[l], ffn_conv_w[l], w_ffn_down[l])
    return rms_norm(x, final_norm_w)
```

```python
import numpy as np
import ml_dtypes
from contextlib import ExitStack
import concourse.bass as bass
import concourse.mybir as mybir
from concourse.bass_utils import run_bass_kernel_spmd

F32 = mybir.dt.float32
BF16 = mybir.dt.bfloat16
ALU = mybir.AluOpType
AF = mybir.ActivationFunctionType
AX = mybir.AxisListType
NPBF = ml_dtypes.bfloat16

ENGS = ("pe", "act", "dve", "pool", "sp")
EPS = 1e-6
SEQ = 4096
DM = 1024
DFF = 2816


class Buf:
    __slots__ = ("name", "w", "r")

    def __init__(self, name):
        self.name = name
        self.w = []
        self.r = []


class _Rec:
    def __init__(self):
        self.call = None

    def __getattr__(self, name):
        def f(*a, **kw):
            self.call = (name, a, kw)
        return f


class Sched:
    def __init__(self, nc, stack):
        self.nc = nc
        self.stack = stack
        self.eng = {"pe": nc.tensor, "act": nc.scalar, "dve": nc.vector,
                    "pool": nc.gpsimd, "sp": nc.sync}
        self.ops = []
        self.n_eng = {e: 0 for e in ENGS}
        self.sem = {e: stack.enter_context(nc.semaphore("s_" + e)) for e in ENGS}
        self.nch = 0
        self.base = {e: 0 for e in ENGS}
        self.chans = []
        self.bufs = []
        self.tot = dict(n_ops=0, n_wait=0)

    def channel(self, name):
        self.nch += 1
        nm = "%s_%d" % (name, self.nch)
        ch = {"sem": self.stack.enter_context(self.nc.semaphore("d_" + nm)), "cnt": 0, "name": nm}
        self.chans.append(ch)
        return ch

    def barrier(self):
        deps = [("e", e2, self.n_eng[e2] - 1) for e2 in ENGS if self.n_eng[e2] > 0 and e2 != "sp"]
        deps += [("d", ch, ch["cnt"]) for ch in self.chans if ch["cnt"] > 0]
        for e in ENGS:
            idx = self.n_eng[e]
            self.n_eng[e] += 1
            self.ops.append((e, None, list(deps), "w", None, idx))
        self.emit()
        self.ops = []
        self.n_eng = {e: 0 for e in ENGS}
        for b in self.bufs:
            b.w = []
            b.r = []

    def _deps(self, reads, writes):
        deps = []
        for b in reads:
            deps.extend(b.w)
        for b in writes:
            deps.extend(b.w)
            deps.extend(b.r)
        return deps

    def _commit(self, ev, reads, writes):
        for b in reads:
            b.r.append(ev)
        for b in writes:
            b.w = [ev]
            b.r = []

    def op(self, eng, fn, reads=(), writes=()):
        rec = _Rec()
        fn(rec)
        call = rec.call
        assert call is not None
        fn = (lambda E, c=call: getattr(E, c[0])(*c[1], **c[2]))
        reads = [t.b for t in reads]
        writes = [t.b for t in writes]
        deps = self._deps(reads, writes)
        idx = self.n_eng[eng]
        self.n_eng[eng] += 1
        ev = ("e", eng, idx)
        self.ops.append((eng, fn, deps, "c", None, idx))
        self._commit(ev, reads, writes)
        return ev

    def dma(self, q, ch, out, in_, reads=(), writes=(), **kw):
        reads = [t.b for t in reads]
        writes = [t.b for t in writes]
        deps = self._deps(reads, writes)
        idx = self.n_eng[q]
        self.n_eng[q] += 1
        ch["cnt"] += 16
        ev = ("d", ch, ch["cnt"])
        self.ops.append((q, (lambda e, o=out, i=in_, k=kw: e.dma_start(out=o, in_=i, **k)), deps, "d", ch, idx))
        self._commit(ev, reads, writes)
        return ev

    def emit(self, final_wait=()):
        know = {e: {} for e in ENGS}
        snap = {}
        plan = []
        targets = {e: set() for e in ENGS}
        for (eng, fn, deps, kind, ch, idx) in self.ops:
            k = know[eng]
            need = {}
            for ev in deps:
                if ev[0] == "e":
                    _, e2, i2 = ev
                    if e2 == eng and eng == "pe":
                        continue
                    if k.get(e2, -1) >= i2:
                        continue
                    if need.get(e2, -1) < i2:
                        need[e2] = i2
                else:
                    _, c2, v2 = ev
                    key = "ch:" + c2["name"]
                    if k.get(key, 0) >= v2:
                        continue
                    if need.get(key, (None, 0))[1] < v2:
                        need[key] = (c2, v2)
            waits = []
            for key, v in need.items():
                if isinstance(v, tuple):
                    waits.append(("d", v[0], v[1]))
                    k[key] = v[1]
                else:
                    waits.append(("e", key, v))
                    targets[key].add(v)
                    k[key] = max(k.get(key, -1), v)
                    for kk, vv in snap[(key, v)].items():
                        if k.get(kk, -1) < vv:
                            k[kk] = vv
            snap[(eng, idx)] = dict(k)
            plan.append(waits)
        rank = {}
        for e in ENGS:
            for r, i in enumerate(sorted(targets[e])):
                rank[(e, i)] = self.base[e] + r + 1
            self.base[e] += len(targets[e])
        n_wait = 0
        for (eng, fn, deps, kind, ch, idx), waits in zip(self.ops, plan):
            E = self.eng[eng]
            for w in waits:
                if w[0] == "d":
                    E.wait_ge(w[1]["sem"], w[2])
                else:
                    E.wait_ge(self.sem[w[1]], rank[(w[1], w[2])])
                n_wait += 1
            if fn is None:
                continue
            ins = fn(E)
            if kind == "d":
                ins.then_inc(ch["sem"], 16)
            elif (eng, idx) in rank:
                ins.then_inc(self.sem[eng], 1)
        E = self.eng["sp"]
        for ev in final_wait:
            E.wait_ge(ev[1]["sem"], ev[2])
        self.tot["n_ops"] += len(self.ops)
        self.tot["n_wait"] += n_wait
        return dict(self.tot)


class T:
    def __init__(self, t, name):
        self.t = t
        self.b = Buf(name)

    def __getitem__(self, k):
        return self.t[k]


class KB:
    def __init__(self):
        self.nc = bass.Bass("TRN2", target_bir_lowering=False)
        self.st = ExitStack()
        self.S = Sched(self.nc, self.st)
        self._ps = []
        self._psi = 0
        self.outs = []
        self._tch = {}

    def dram_in(self, name, shape, dt):
        return self.nc.dram_tensor(name, list(shape), dt, kind="ExternalInput").ap()

    def dram_out(self, name, shape, dt):
        return self.nc.dram_tensor(name, list(shape), dt, kind="ExternalOutput").ap()

    def sb(self, name, shape, dt, scope=None):
        t = T((scope or self.st).enter_context(self.nc.sbuf_tensor("sb_" + name, list(shape), dt)), name)
        self.S.bufs.append(t.b)
        return t

    def psum(self, name, shape, dt, scope=None):
        t = T((scope or self.st).enter_context(self.nc.psum_tensor("ps_" + name, list(shape), dt)), name)
        self.S.bufs.append(t.b)
        return t

    def mkpool(self, n, scope=None, tag=""):
        self._ps = [self.psum("psb%s%d" % (tag, i), [128, 512], F32, scope=scope) for i in range(n)]
        self._psi = 0

    def ps(self):
        p = self._ps[self._psi % len(self._ps)]
        self._psi += 1
        return p

    def op(self, eng, fn, reads=(), writes=()):
        return self.S.op(eng, fn, reads, writes)

    def _chan(self, t, kind):
        key = (id(t.b), kind)
        if key not in self._tch:
            self._tch[key] = self.S.channel(kind + t.b.name)
        return self._tch[key]

    def load(self, ch, dst, out_ap, in_ap, q="sp"):
        return self.S.dma(q, self._chan(dst, "l"), out_ap, in_ap, writes=[dst])

    def store(self, ch, src, out_ap, in_ap, q="sp"):
        ev = self.S.dma(q, self._chan(src, "s"), out_ap, in_ap, reads=[src])
        self.outs.append(ev)
        return ev

    def finish(self):
        last = {}
        for ev in self.outs:
            last[ev[1]["name"]] = ev
        stats = self.S.emit(final_wait=list(last.values()))
        self.st.close()
        return stats


class WStream:
    def __init__(self, k, dram, order, widths, nslots=4, slotw=5632):
        self.k = k
        self.dram = dram
        self.order = order
        self.widths = widths
        self.slots = [k.sb("wslot%d" % i, [128, slotw], BF16) for i in range(nslots)]
        self.chs = [k.S.channel("w%d" % i) for i in range(nslots)]
        self.issued = 0
        self.pos = 0

    def _issue(self):
        i = self.issued
        g = self.order[i]
        s = i % len(self.slots)
        w = self.widths[g]
        self.k.load(self.chs[s], self.slots[s], self.slots[s][:, :w], self.dram[g, :, :w])
        self.issued += 1

    def next(self):
        while self.issued < len(self.order) and self.issued < self.pos + len(self.slots):
            self._issue()
        s = self.slots[self.pos % len(self.slots)]
        self.pos += 1
        return s


def rmsnorm_fm(k, xt, nw, hT, N, ones, epsT, sq, rstd, ssps, ho=0):
    for kc in range(8):
        s = sq[kc % 2]
        k.op("act", lambda e, s=s, kc=kc: e.activation(out=s[:, :N], in_=xt[:, kc, :N], func=AF.Square),
             reads=[xt], writes=[s])
        k.op("pe", lambda e, s=s, kc=kc: e.matmul(ssps[:, :N], lhsT=ones[:], rhs=s[:, :N], start=(kc == 0), stop=(kc == 7)),
             reads=[s, ones], writes=[ssps])
    k.op("act", lambda e: e.activation(out=rstd[:, :N], in_=ssps[:, :N], func=AF.Sqrt, scale=1.0 / DM, bias=epsT[:, 0:1]),
         reads=[ssps, epsT], writes=[rstd])
    k.op("dve", lambda e: e.reciprocal(out=rstd[:, :N], in_=rstd[:, :N]), reads=[rstd], writes=[rstd])
    for kc in range(8):
        k.op("dve", lambda e, kc=kc: e.scalar_tensor_tensor(out=hT[:, kc, ho:ho + N], in0=xt[:, kc, :N], scalar=nw[:, kc:kc + 1],
                                                            in1=rstd[:, :N], op0=ALU.mult, op1=ALU.mult),
             reads=[xt, nw, rstd], writes=[hT])


class Sub:
    def __init__(self, parent, off):
        self.p = parent
        self.b = parent.b
        self.off = off

    def __getitem__(self, key):
        sl = key[1]
        return self.p.t[:, self.off + sl.start: self.off + sl.stop]


B_NT = 2176
B_TILES = [(0, 128)] + [(128 + 512 * i, 512) for i in range(4)]
B_NG = 39
B_SLOTW = 4096


def build_B(final, dbg=None):
    k = KB()
    xT = k.dram_in("xT", [128, 8, B_NT], F32)
    yrT = k.dram_in("yrT", [128, 8, B_NT], BF16)
    ynT = k.dram_in("ynT", [128, 8, B_NT], BF16)
    wst = k.dram_in("wst", [B_NG, 128, B_SLOTW], BF16)
    vecs = k.dram_in("vecs", [128, 180], F32)
    xo = k.dram_out("xo", [128, 8, 2048], F32)

    k.mkpool(7)
    ssps = k.psum("ssps", [128, 512], F32)
    xt = k.sb("xt", [128, 8, 512], F32)
    hT = k.sb("hT", [128, 8, 512], BF16)
    sq = [k.sb("sq%d" % i, [128, 512], F32) for i in range(2)]
    rstd = k.sb("rstd", [128, 512], F32)
    SBt = k.sb("SBt", [128, 8, 512], BF16)
    SCt = k.sb("SCt", [128, 8, 512], BF16)
    G, mrgb = SCt, SBt
    U = k.sb("U", [128, 8, 514], F32)
    ysc = k.sb("ysc", [128, 8, 512], BF16)
    yld = k.sb("yld", [128, 8, 512], BF16)
    mrg = k.sb("mrg", [128, 8, 512], F32)
    tmp = [k.sb("tmp%d" % i, [128, 512], F32) for i in range(2)]
    A = k.sb("A", [128, 22, 512], BF16)
    ut = [k.sb("ut%d" % i, [128, 514], F32) for i in range(3)]
    cv = [k.sb("cv%d" % i, [128, 512], F32) for i in range(2)]
    CF = k.sb("CF", [128, 44, 2], F32)
    U2 = k.sb("U2", [128, 8, 2], F32)
    vt = k.sb("vt", [128, 180], F32)
    ones = k.sb("ones", [128, 128], F32)
    epsT = k.sb("epsT", [128, 1], F32)
    cx, cy, cvv, co = k.S.channel("x"), k.S.channel("y"), k.S.channel("v"), k.S.channel("o")

    k.load(cvv, vt, vt[:], vecs)
    k.op("pool", lambda e: e.memset(ones[:], 1.0), writes=[ones])
    k.op("pool", lambda e: e.memset(epsT[:], EPS), writes=[epsT])
    k.op("pool", lambda e: e.memset(U[:], 0.0), writes=[U])
    k.op("pool", lambda e: e.memset(CF[:], 0.0), writes=[CF])

    order = []
    for (t0, N) in B_TILES:
        order += list(range(0, 31 if N == 128 else 39))
    ws = WStream(k, wst, order, [B_SLOTW] * 31 + [2816] * 8, nslots=4, slotw=B_SLOTW)

    scw = lambda c, j: vt[:, 24 + c * 3 + j:24 + c * 3 + j + 1]
    fcw = lambda c, j: vt[:, 48 + c * 3 + j:48 + c * 3 + j + 1]

    def proj_group(wslot, rhsT, N, nk=8, cols=512):
        for c in range(cols // 128):
            p = k.ps()
            for kc in range(nk):
                k.op("pe", lambda e, p=p, c=c, kc=kc: e.matmul(p[:, :N], lhsT=wslot[:, kc * cols + c * 128: kc * cols + (c + 1) * 128],
                                                                rhs=rhsT[:, kc, :N], start=(kc == 0), stop=(kc == nk - 1)),
                     reads=[wslot, rhsT], writes=[p])
            yield c, p

    def tile_body(ti, t0, N):
        halo = (N == 128)
        k.load(cx, xt, xt[:, :, :N], xT[:, :, t0:t0 + N])
        rmsnorm_fm(k, xt, Sub(vt, 0), hT, N, ones, epsT, sq, rstd, ssps)
        if dbg == "h" and not halo:
            k.op("dve", lambda e: e.tensor_copy(out=mrg[:, :, :N], in_=hT[:, :, :N]), reads=[hT], writes=[mrg])
            k.store(co, mrg, xo[:, :, t0 - 128:t0 - 128 + N], mrg[:, :, :N])
            return
        for g in range(6):
            wslot = ws.next()
            for c, p in proj_group(wslot, hT, N):
                ch = (g % 2) * 4 + c
                if g < 2:
                    k.op("act", lambda e, p=p, ch=ch: e.copy(out=SBt[:, ch, :N], in_=p[:, :N]), reads=[p], writes=[SBt])
                elif g < 4:
                    k.op("act", lambda e, p=p, ch=ch: e.copy(out=SCt[:, ch, :N], in_=p[:, :N]), reads=[p], writes=[SCt])
                else:
                    k.op("dve", lambda e, p=p, ch=ch: e.tensor_tensor(out=U[:, ch, 2:2 + N], in0=p[:, :N], in1=SCt[:, ch, :N], op=ALU.mult),
                         reads=[p, SCt], writes=[U])
        for ch in range(8):
            c0 = cv[ch % 2]
            k.op("dve", lambda e, c0=c0, ch=ch: e.tensor_scalar(out=c0[:, :N], in0=U[:, ch, 2:2 + N], scalar1=scw(ch, 2), scalar2=None, op0=ALU.mult),
                 reads=[U, vt], writes=[c0])
            k.op("dve", lambda e, c0=c0, ch=ch: e.scalar_tensor_tensor(out=c0[:, :N], in0=U[:, ch, 1:1 + N], scalar=scw(ch, 1), in1=c0[:, :N], op0=ALU.mult, op1=ALU.add),
                 reads=[U, vt, c0], writes=[c0])
            k.op("dve", lambda e, c0=c0, ch=ch: e.scalar_tensor_tensor(out=c0[:, :N], in0=U[:, ch, 0:N], scalar=scw(ch, 0), in1=c0[:, :N], op0=ALU.mult, op1=ALU.add),
                 reads=[U, vt, c0], writes=[c0])
            k.op("pool", lambda e, c0=c0, ch=ch: e.tensor_tensor(out=ysc[:, ch, :N], in0=c0[:, :N], in1=SBt[:, ch, :N], op=ALU.mult),
                 reads=[c0, SBt], writes=[ysc])
        k.op("pool", lambda e: e.tensor_copy(out=U2[:], in_=U[:, :, N:N + 2]), reads=[U], writes=[U2])
        k.op("pool", lambda e: e.tensor_copy(out=U[:, :, 0:2], in_=U2[:]), reads=[U2], writes=[U])
        if dbg == "ysc" and not halo:
            k.op("dve", lambda e: e.tensor_copy(out=mrg[:, :, :N], in_=ysc[:, :, :N]), reads=[ysc], writes=[mrg])
            k.store(co, mrg, xo[:, :, t0 - 128:t0 - 128 + N], mrg[:, :, :N])
            return
        for br in range(3):
            for g in range(2):
                wslot = ws.next()
                for c, p in proj_group(wslot, hT, N):
                    ch = g * 4 + c
                    k.op("act", lambda e, p=p, ch=ch: e.activation(out=G[:, ch, :N], in_=p[:, :N], func=AF.Sigmoid), reads=[p], writes=[G])
            if br == 1:
                src = ysc
            else:
                src = yld
                k.load(cy, yld, yld[:, :, :N], (yrT if br == 0 else ynT)[:, :, t0:t0 + N])
            for g in range(2):
                wslot = ws.next()
                for c, p in proj_group(wslot, src, N):
                    ch = g * 4 + c
                    if br == 0:
                        k.op("dve", lambda e, p=p, ch=ch: e.tensor_tensor(out=mrg[:, ch, :N], in0=p[:, :N], in1=G[:, ch, :N], op=ALU.mult),
                             reads=[p, G], writes=[mrg])
                    else:
                        tm = tmp[ch % 2]
                        dst = mrg if br == 1 else mrgb
                        k.op("dve", lambda e, p=p, ch=ch, tm=tm: e.tensor_tensor(out=tm[:, :N], in0=p[:, :N], in1=G[:, ch, :N], op=ALU.mult),
                             reads=[p, G], writes=[tm])
                        k.op("pool", lambda e, ch=ch, tm=tm, dst=dst: e.tensor_tensor(out=dst[:, ch, :N], in0=mrg[:, ch, :N], in1=tm[:, :N], op=ALU.add),
                             reads=[mrg, tm], writes=[dst])
        for g in range(2):
            wslot = ws.next()
            for c, p in proj_group(wslot, mrgb, N):
                ch = g * 4 + c
                k.op("dve", lambda e, p=p, ch=ch: e.tensor_tensor(out=xt[:, ch, :N], in0=p[:, :N], in1=xt[:, ch, :N], op=ALU.add),
                     reads=[p, xt], writes=[xt])
        if dbg == "x1" and not halo:
            k.store(co, xt, xo[:, :, t0 - 128:t0 - 128 + N], xt[:, :, :N])
            return
        rmsnorm_fm(k, xt, Sub(vt, 8), hT, N, ones, epsT, sq, rstd, ssps)
        for g in range(11):
            wslot = ws.next()
            for c, p in proj_group(wslot, hT, N):
                ch = g * 4 + c
                u = ut[ch % 3]
                k.op("pool", lambda e, u=u, ch=ch: e.tensor_copy(out=u[:, 0:2], in_=CF[:, ch, :]), reads=[CF], writes=[u])
                k.op("act", lambda e, u=u, p=p: e.copy(out=u[:, 2:2 + N], in_=p[:, :N]), reads=[p], writes=[u])
                k.op("pool", lambda e, u=u, ch=ch: e.tensor_copy(out=CF[:, ch, :], in_=u[:, N:N + 2]), reads=[u], writes=[CF])
                if halo:
                    continue
                c0 = cv[ch % 2]
                k.op("dve", lambda e, c0=c0, ch=ch, u=u: e.tensor_scalar(out=c0[:, :N], in0=u[:, 2:2 + N], scalar1=fcw(ch, 2), scalar2=None, op0=ALU.mult),
                     reads=[u, vt], writes=[c0])
                k.op("dve", lambda e, c0=c0, ch=ch, u=u: e.scalar_tensor_tensor(out=c0[:, :N], in0=u[:, 1:1 + N], scalar=fcw(ch, 1), in1=c0[:, :N], op0=ALU.mult, op1=ALU.add),
                     reads=[u, vt, c0], writes=[c0])
                k.op("dve", lambda e, c0=c0, ch=ch, u=u: e.scalar_tensor_tensor(out=c0[:, :N], in0=u[:, 0:N], scalar=fcw(ch, 0), in1=c0[:, :N], op0=ALU.mult, op1=ALU.add),
                     reads=[u, vt, c0], writes=[c0])
                if ch < 22:
                    k.op("act", lambda e, c0=c0, ch=ch: e.activation(out=A[:, ch, :N], in_=c0[:, :N], func=AF.Silu), reads=[c0], writes=[A])
                else:
                    k.op("pool", lambda e, c0=c0, ch=ch: e.tensor_tensor(out=A[:, ch - 22, :N], in0=A[:, ch - 22, :N], in1=c0[:, :N], op=ALU.mult),
                         reads=[c0, A], writes=[A])
        if halo:
            return
        for g in range(8):
            wslot = ws.next()
            for c, p in proj_group(wslot, A, N, nk=22, cols=128):
                k.op("dve", lambda e, p=p, g=g: e.tensor_tensor(out=xt[:, g, :N], in0=p[:, :N], in1=xt[:, g, :N], op=ALU.add),
                     reads=[p, xt], writes=[xt])
        if final:
            rmsnorm_fm(k, xt, Sub(vt, 16), hT, N, ones, epsT, sq, rstd, ssps) if False else None
            for kc in range(8):
                s = sq[kc % 2]
                k.op("act", lambda e, s=s, kc=kc: e.activation(out=s[:, :N], in_=xt[:, kc, :N], func=AF.Square), reads=[xt], writes=[s])
                k.op("pe", lambda e, s=s, kc=kc: e.matmul(ssps[:, :N], lhsT=ones[:], rhs=s[:, :N], start=(kc == 0), stop=(kc == 7)),
                     reads=[s, ones], writes=[ssps])
            k.op("act", lambda e: e.activation(out=rstd[:, :N], in_=ssps[:, :N], func=AF.Sqrt, scale=1.0 / DM, bias=epsT[:, 0:1]),
                 reads=[ssps, epsT], writes=[rstd])
            k.op("dve", lambda e: e.reciprocal(out=rstd[:, :N], in_=rstd[:, :N]), reads=[rstd], writes=[rstd])
            for kc in range(8):
                k.op("dve", lambda e, kc=kc: e.scalar_tensor_tensor(out=xt[:, kc, :N], in0=xt[:, kc, :N], scalar=vt[:, 16 + kc:17 + kc],
                                                                    in1=rstd[:, :N], op0=ALU.mult, op1=ALU.mult),
                     reads=[xt, vt, rstd], writes=[xt])
        k.store(co, xt, xo[:, :, t0 - 128:t0 - 128 + N], xt[:, :, :N])

    for ti, (t0, N) in enumerate(B_TILES):
        tile_body(ti, t0, N)
    return k


PREP_CH = 4096


def build_prep(M):
    k = KB()
    src = k.dram_in("wsrc", [128, M], F32)
    dst = k.dram_out("wdst", [128, M], BF16)
    st = [k.sb("pst%d" % i, [128, PREP_CH], F32) for i in range(3)]
    ob = [k.sb("pob%d" % i, [128, PREP_CH], BF16) for i in range(3)]
    ci = [k.S.channel("pi%d" % i) for i in range(3)]
    co = [k.S.channel("po%d" % i) for i in range(3)]
    n = M // PREP_CH
    for i in range(n):
        s = i % 3
        k.load(ci[s], st[s], st[s][:], src[:, i * PREP_CH:(i + 1) * PREP_CH])
        eng = ("dve", "act", "pool")[i % 3]
        if eng == "act":
            k.op("act", lambda e, s=s: e.copy(out=ob[s][:], in_=st[s][:]), reads=[st[s]], writes=[ob[s]])
        else:
            k.op(eng, lambda e, s=s: e.tensor_copy(out=ob[s][:], in_=st[s][:]), reads=[st[s]], writes=[ob[s]])
        k.store(co[s], ob[s], dst[:, i * PREP_CH:(i + 1) * PREP_CH], ob[s][:])
    return k


_CACHE = {}


def _get(name, fn):
    if name not in _CACHE:
        k = fn()
        k.finish()
        _CACHE[name] = k.nc
    return _CACHE[name]


W_NAMES = ["w_in", "w_ret_out", "w_sc_out", "nsa_cmp_w1", "nsa_cmp_w2", "w_nsa_out", "w_mix_out", "w_ffn_up", "w_ffn_down"]


def prep_weights(inputs):
    flats = [np.ascontiguousarray(inputs[n], dtype=np.float32).reshape(-1) for n in W_NAMES]
    tot = sum(f.size for f in flats)
    unit = 8 * 128 * PREP_CH
    pad = (-tot) % unit
    blob = np.concatenate(flats + [np.zeros(pad, np.float32)])
    M = blob.size // (8 * 128)
    blob = blob.reshape(8, 128, M)
    nc = _get("prep%d" % M, lambda: build_prep(M))
    res = run_bass_kernel_spmd(nc, [{"wsrc": blob[i]} for i in range(8)], core_ids=list(range(8)))
    out = np.concatenate([np.asarray(r["wdst"]).reshape(-1) for r in res.results])
    W = {}
    o = 0
    for n, f in zip(W_NAMES, flats):
        W[n] = out[o:o + f.size].reshape(inputs[n].shape)
        o += f.size
    return W


def fm(a):
    K, C = a.shape
    return np.ascontiguousarray(a.reshape(K // 128, 128, C).transpose(1, 0, 2))


def grp(a, cols):
    K, C = a.shape
    f = fm(a)
    return [np.ascontiguousarray(f[:, :, c0:c0 + cols]).reshape(128, -1) for c0 in range(0, C, cols)]


IN_OFF = dict(rq=0, rk=512, rv=1024, rg=2048, sb=3072, sc=4096, sx=5120, nq=6144, nkv=7168, ng=8704, mg=8752)


def B_weights(W, inputs, l):
    w_in = W["w_in"][l]
    gs = []
    for nm in ("sb", "sc", "sx"):
        gs += grp(w_in[:, IN_OFF[nm]:IN_OFF[nm] + 1024], 512)
    outs = [W["w_ret_out"][l], W["w_sc_out"][l], W["w_nsa_out"][l]]
    for br in range(3):
        gs += grp(w_in[:, IN_OFF["mg"] + 1024 * br: IN_OFF["mg"] + 1024 * (br + 1)], 512)
        gs += grp(outs[br], 512)
    gs += grp(W["w_mix_out"][l], 512)
    gs += grp(W["w_ffn_up"][l], 512)
    dn = grp(W["w_ffn_down"][l], 128)
    wst = np.zeros((B_NG, 128, B_SLOTW), NPBF)
    for i, g in enumerate(gs + dn):
        wst[i, :, :g.shape[1]] = g
    vecs = np.zeros((128, 180), np.float32)
    vecs[:, 0:8] = inputs["attn_norm_w"][l].reshape(8, 128).T
    vecs[:, 8:16] = inputs["ffn_norm_w"][l].reshape(8, 128).T
    vecs[:, 16:24] = inputs["final_norm_w"].reshape(8, 128).T
    vecs[:, 24:48] = inputs["sc_conv_w"][l].reshape(3, 8, 128).transpose(2, 1, 0).reshape(128, 24)
    vecs[:, 48:180] = inputs["ffn_conv_w"][l].reshape(3, 44, 128).transpose(2, 1, 0).reshape(128, 132)
    return wst, vecs


def run_B(W, inputs, l, xT_full, yrT_full, ynT_full, final, dbg=None):
    wst, vecs = B_weights(W, inputs, l)
    nc = _get("B%d%s" % (int(final), dbg), lambda: build_B(final, dbg))
    maps = []
    for b in range(4):
        for sh in range(2):
            t0 = sh * 2048 - 128

            def cut(a, dt):
                o = np.zeros((1024, B_NT), dt)
                lo = max(t0, 0)
                o[:, lo - t0:] = a[b][:, lo:t0 + B_NT]
                return fm(o)
            maps.append({"xT": cut(xT_full, np.float32), "yrT": cut(yrT_full, NPBF), "ynT": cut(ynT_full, NPBF),
                         "wst": wst, "vecs": vecs})
    res = run_bass_kernel_spmd(nc, maps, core_ids=list(range(8)))
    out = np.zeros_like(xT_full)
    for b in range(4):
        for sh in range(2):
            r = np.asarray(res.results[b * 2 + sh]["xo"])
            out[b][:, sh * 2048:(sh + 1) * 2048] = r.transpose(1, 0, 2).reshape(1024, 2048)
    return out


BIG = 16384.0
GELU_C = 1.5957691216057308
NSA_SCALE = 0.125


def build_A(dbg=None):
    k = KB()
    nc = k.nc
    S = k.S
    xT = k.dram_in("xT", [128, 8, SEQ], F32)
    nwA = k.dram_in("nwA", [128, 8], F32)
    w_fm_d = k.dram_in("w_fm", [128, 8, 13 * 128], BF16)
    w_ks_d = k.dram_in("w_ks", [128, 8, 128], BF16)
    w_tm_d = k.dram_in("w_tm", [128, 8, 1280], BF16)
    w_q_d = k.dram_in("w_q", [128, 8, 1048], BF16)
    w1_d = k.dram_in("w1", [128, 2, 16, 256], BF16)
    w2_d = k.dram_in("w2", [128, 2, 2, 64], BF16)
    posb_d = k.dram_in("posb", [128, 2, 16], F32)
    cs_d = k.dram_in("cs", [128, 2, SEQ], F32)
    rc_d = k.dram_in("rc", [128, 2, 3, 128], F32)
    gd_d = k.dram_in("gd", [128, 2], F32)
    rnw_d = k.dram_in("rnw", [128, 512], F32)
    kaE_d = k.dram_in("kaE", [64, SEQ], BF16)
    trib_d = k.dram_in("trib", [128, 2, 512], BF16)
    ident_d = k.dram_in("ident", [128, 128], BF16)
    cm_d = k.dram_in("cm", [32, 128, 2, 128], BF16)
    selc_d = k.dram_in("selc", [32, 128, 3, 64], F32)
    ovl_d = k.dram_in("ovl", [128, 2, 65], BF16)
    yT = k.dram_out("yT", [128, 8, SEQ], BF16)

    Kaug = k.sb("Kaug", [128, 2, SEQ], BF16)
    KWc = k.sb("KWc", [128, SEQ], BF16)
    Vs = k.sb("Vs", [128, 32, 2, 65], BF16)
    Vw = k.sb("Vw", [128, 32, 2, 65], BF16)
    KCc = k.sb("KCc", [128, 2, 260], BF16)
    VCaug = k.sb("VCaug", [128, 2, 2, 129], BF16)
    xt = k.sb("xt", [128, 8, 256], F32)
    hT = k.sb("hT", [128, 8, 512], BF16)
    sq = [k.sb("sq%d" % i, [128, 256], F32) for i in range(2)]
    rstd = k.sb("rstd", [128, 256], F32)
    nw = k.sb("nw", [128, 8], F32)
    ones = k.sb("ones", [128, 128], F32)
    epsT = k.sb("epsT", [128, 1], F32)
    ident = k.sb("ident", [128, 128], BF16)
    ybf = k.sb("ybf", [128, 512], BF16)
    YTs = [k.sb("YTs%d" % i, [128, 4, 512], BF16) for i in range(2)]
    TRPall = k.psum("TRPall", [128, 1024], BF16)
    TRP = T(TRPall[:, 0:512], "TRP")
    TRP2 = T(TRPall[:, 512:1024], "TRP2")
    S.bufs.extend([TRP.b, TRP2.b])
    cx, cc, cw, co = S.channel("x"), S.channel("c"), S.channel("w"), S.channel("o")
    cK = S.channel("K")

    k.load(cc, nw, nw[:], nwA)
    k.load(cc, ident, ident[:], ident_d)
    for g in range(2):
        k.load(cK, Kaug, Kaug[64:128, g, :], kaE_d)
    k.op("pool", lambda e: e.memset(ones[:], 1.0), writes=[ones])
    k.op("pool", lambda e: e.memset(epsT[:], EPS), writes=[epsT])
    k.op("pool", lambda e: e.memset(Vs[:], 1.0), writes=[Vs])
    k.op("pool", lambda e: e.memset(Vw[:], 1.0), writes=[Vw])
    k.op("pool", lambda e: e.memset(KCc[:], 0.0), writes=[KCc])
    for g in range(2):
        k.load(cc, VCaug, VCaug[:, :, g, 64:129], ovl_d)

    def load_norm(t0):
        for hf in range(2):
            k.load(cx, xt, xt[:], xT[:, :, t0 + 256 * hf: t0 + 256 * hf + 256])
            rmsnorm_fm(k, xt, nw, hT, 256, ones, epsT, sq, rstd, k.ps(), ho=256 * hf)

    def transposes_out(src, dstY, s):
        for c in range(4):
            k.op("pe", lambda e, c=c: e.transpose(TRP2[:, c * 128:(c + 1) * 128], src[:, c * 128:(c + 1) * 128], ident[:]),
                 reads=[src, ident], writes=[TRP2])
        k.op("act", lambda e: e.copy(out=dstY[:, :, s * 128:(s + 1) * 128], in_=TRP2[:].rearrange("p (c q) -> p c q", c=4)),
             reads=[TRP2], writes=[dstY])

    sc1 = ExitStack()
    k.mkpool(5, scope=sc1, tag="a")
    PO = k.psum("PO", [128, 512], F32, scope=sc1)
    w_fm = k.sb("w_fm", [128, 8, 13 * 128], BF16, scope=sc1)
    w_ks = k.sb("w_ks", [128, 8, 128], BF16, scope=sc1)
    w_tm = k.sb("w_tm", [128, 8, 1280], BF16, scope=sc1)
    w1 = k.sb("w1", [128, 2, 16, 256], BF16, scope=sc1)
    w2 = k.sb("w2", [128, 2, 2, 64], BF16, scope=sc1)
    posf = k.sb("posf", [128, 2, 16], F32, scope=sc1)
    posb = k.sb("posb", [128, 2, 16], BF16, scope=sc1)
    PBf = k.sb("PBf", [128, 8, 32], F32, scope=sc1)
    cst = k.sb("cst", [128, 2, 512], F32, scope=sc1)
    rc = k.sb("rc", [128, 2, 3, 128], F32, scope=sc1)
    gd = k.sb("gd", [128, 2], F32, scope=sc1)
    rnw = k.sb("rnw", [128, 512], F32, scope=sc1)
    QR = k.sb("QR", [128, 2, 512], BF16, scope=sc1)
    KR = k.sb("KR", [128, 2, 512], BF16, scope=sc1)
    t1 = k.sb("t1", [128, 512], F32, scope=sc1)
    t2 = k.sb("t2", [128, 512], F32, scope=sc1)
    KCL = k.sb("KCL", [128, 4, 530], BF16, scope=sc1)
    GH = k.sb("GH", [128, 8, 260], BF16, scope=sc1)
    gx = [k.sb("gx%d" % i, [128, 8, 32], F32, scope=sc1) for i in range(3)]
    Vr = [k.sb("Vr%d" % i, [128, 512], BF16, scope=sc1) for i in range(2)]
    SG = k.sb("SG", [128, 512], F32, scope=sc1)
    AT = [k.sb("AT%d" % i, [128, 128], BF16, scope=sc1) for i in range(2)]
    QX = [k.sb("QX%d" % i, [128, 128], BF16, scope=sc1) for i in range(2)]
    KZ = [k.sb("KZ%d" % i, [128, 128], BF16, scope=sc1) for i in range(2)]
    KZT = [k.sb("KZT%d" % i, [128, 128], BF16, scope=sc1) for i in range(2)]
    R = k.sb("R", [128, 2, 256], F32, scope=sc1)
    Rbf = k.sb("Rbf", [128, 2, 256], BF16, scope=sc1)
    st6 = k.sb("st6", [128, 2, 6], F32, scope=sc1)
    mv = k.sb("mv", [128, 2, 2], F32, scope=sc1)
    rs = k.sb("rs", [128, 2], F32, scope=sc1)
    yn = [k.sb("yn%d" % i, [128, 256], F32, scope=sc1) for i in range(2)]

    k.load(cw, w_fm, w_fm[:], w_fm_d)
    k.load(cw, w_ks, w_ks[:], w_ks_d)
    k.load(cw, w_tm, w_tm[:], w_tm_d)
    k.load(cw, w1, w1[:], w1_d)
    k.load(cw, w2, w2[:], w2_d)
    k.load(cc, posf, posf[:], posb_d)
    k.load(cc, rc, rc[:], rc_d)
    k.load(cc, gd, gd[:], gd_d)
    k.load(cc, rnw, rnw[:], rnw_d)
    k.op("dve", lambda e: e.tensor_copy(out=posb[:], in_=posf[:]), reads=[posf], writes=[posb])
    k.op("pool", lambda e: e.memset(KCL[:], 0.0), writes=[KCL])
    k.op("pool", lambda e: e.memset(GH[:], 0.0), writes=[GH])
    k.op("pool", lambda e: e.memset(R[:], 0.0), writes=[R])
    k.op("pool", lambda e: e.memset(Rbf[:], 0.0), writes=[Rbf])
    pb = k.ps()
    for kv in range(2):
        for hc in range(2):
            for m in range(16):
                k.op("pe", lambda e, kv=kv, hc=hc, m=m: e.matmul(pb[:, kv * 2 + hc: kv * 2 + hc + 1], lhsT=w1[:, kv, m, hc * 128:(hc + 1) * 128],
                                                                  rhs=posb[:, kv, m:m + 1], start=(m == 0), stop=(m == 15)),
                     reads=[w1, posb], writes=[pb])
    for j in range(4):
        for hc in range(2):
            kv = j // 2
            k.op("dve", lambda e, j=j, hc=hc, kv=kv: e.tensor_copy(out=PBf[:, j * 2 + hc, :], in_=pb[:, kv * 2 + hc: kv * 2 + hc + 1].broadcast_to([128, 32])),
                 reads=[pb], writes=[PBf])

    def fm_chunk(c):
        p = k.ps()
        for kc in range(8):
            k.op("pe", lambda e, kc=kc: e.matmul(p[:, :512], lhsT=w_fm[:, kc, c * 128:(c + 1) * 128], rhs=hT[:, kc, :],
                                                 start=(kc == 0), stop=(kc == 7)), reads=[w_fm, hT], writes=[p])
        return p

    def ret_chunk(i, s):
        v = Vr[s % 2]
        sl = slice(s * 128, (s + 1) * 128)
        for h in range(2):
            pst = k.ps()
            k.op("pe", lambda e: e.matmul(pst[:, :128], lhsT=KR[:, h, sl], rhs=QR[:, h, sl], start=True, stop=True), reads=[KR, QR], writes=[pst])
            k.op("dve", lambda e: e.tensor_tensor(out=AT[h][:], in0=pst[:, :128], in1=rc[:, h, 0, :], op=ALU.mult), reads=[pst, rc], writes=[AT[h]])
            k.op("pool", lambda e: e.tensor_tensor(out=QX[h][:], in0=QR[:, h, sl], in1=rc[:, h, 1, :], op=ALU.mult), reads=[QR, rc], writes=[QX[h]])
            k.op("pe", lambda e: e.matmul(PO[:, h * 256:(h + 1) * 256], lhsT=AT[h][:], rhs=v[:, h * 256:(h + 1) * 256], start=True, stop=False),
                 reads=[AT[h], v], writes=[PO])
            k.op("pe", lambda e: e.matmul(PO[:, h * 256:(h + 1) * 256], lhsT=QX[h][:], rhs=Rbf[:, h, :], start=False, stop=True),
                 reads=[QX[h], Rbf], writes=[PO])
            k.op("pool", lambda e: e.tensor_tensor(out=KZ[h][:], in0=KR[:, h, sl], in1=rc[:, h, 2, :], op=ALU.mult), reads=[KR, rc], writes=[KZ[h]])
            k.op("pe", lambda e: e.transpose(TRP[:, h * 128:(h + 1) * 128], KZ[h][:], ident[:]), reads=[KZ[h], ident], writes=[TRP])
            k.op("act", lambda e: e.copy(out=KZT[h][:], in_=TRP[:, h * 128:(h + 1) * 128]), reads=[TRP], writes=[KZT[h]])
            pkv = k.ps()
            k.op("pe", lambda e: e.matmul(pkv[:, :256], lhsT=KZT[h][:], rhs=v[:, h * 256:(h + 1) * 256], start=True, stop=True), reads=[KZT[h], v], writes=[pkv])
            k.op("dve", lambda e: e.scalar_tensor_tensor(out=R[:, h, :], in0=R[:, h, :], scalar=gd[:, h:h + 1], in1=pkv[:, :256], op0=ALU.mult, op1=ALU.add),
                 reads=[R, gd, pkv], writes=[R])
            k.op("act", lambda e: e.copy(out=Rbf[:, h, :], in_=R[:, h, :]), reads=[R], writes=[Rbf])
            k.op("dve", lambda e: e.bn_stats(out=st6[:, h, :], in_=PO[:, h * 256:(h + 1) * 256]), reads=[PO], writes=[st6])
            k.op("dve", lambda e: e.bn_aggr(out=mv[:, h, :], in_=st6[:, h, :]), reads=[st6], writes=[mv])
            k.op("act", lambda e: e.activation(out=rs[:, h:h + 1], in_=mv[:, h, 1:2], func=AF.Sqrt, bias=epsT[:, 0:1]), reads=[mv, epsT], writes=[rs])
            k.op("dve", lambda e: e.reciprocal(out=rs[:, h:h + 1], in_=rs[:, h:h + 1]), reads=[rs], writes=[rs])
            k.op("dve", lambda e: e.tensor_scalar(out=yn[h][:], in0=PO[:, h * 256:(h + 1) * 256], scalar1=mv[:, h, 0:1], scalar2=rs[:, h:h + 1],
                                                  op0=ALU.subtract, op1=ALU.mult), reads=[PO, mv, rs], writes=[yn[h]])
            k.op("pool", lambda e: e.tensor_tensor(out=yn[h][:], in0=yn[h][:], in1=rnw[:, h * 256:(h + 1) * 256], op=ALU.mult), reads=[yn[h], rnw], writes=[yn[h]])
            k.op("pool", lambda e: e.tensor_tensor(out=ybf[:, h * 256:(h + 1) * 256], in0=yn[h][:], in1=SG[:, h * 256:(h + 1) * 256], op=ALU.mult),
                 reads=[yn[h], SG], writes=[ybf])
            yield h
        transposes_out(ybf, YTs[i % 2], s)

    def pass1_tile(i):
        t0 = 512 * i
        load_norm(t0)
        k.load(cc, cst, cst[:], cs_d[:, :, t0:t0 + 512])
        for c in range(13):
            p = fm_chunk(c)
            if c < 8:
                dst = (QR, KR)[c // 4]
                h = (c // 2) % 2
                if c % 2 == 0:
                    k.op("dve", lambda e: e.tensor_tensor(out=t1[:], in0=p[:, :512], in1=cst[:, 0, :], op=ALU.mult), reads=[p, cst], writes=[t1])
                else:
                    k.op("dve", lambda e: e.tensor_tensor(out=t2[:], in0=p[:, :512], in1=cst[:, 1, :], op=ALU.mult), reads=[p, cst], writes=[t2])
                    k.op("pool", lambda e: e.tensor_tensor(out=dst[:, h, :], in0=t1[:], in1=t2[:], op=ALU.add), reads=[t1, t2], writes=[dst])
            elif c < 12:
                j = c - 8
                k.op("act", lambda e: e.copy(out=KCL[0:64, j, 17:529], in_=p[0:64, :512]), reads=[p], writes=[KCL])
                k.op("act", lambda e: e.copy(out=KCL[64:128, j, 16:528], in_=p[64:128, :512]), reads=[p], writes=[KCL])
            else:
                k.op("act", lambda e: e.copy(out=KWc[:, t0:t0 + 512], in_=p[:, :512]), reads=[p], writes=[KWc])
        for g in range(2):
            p = k.ps()
            for kc in range(8):
                k.op("pe", lambda e, kc=kc: e.matmul(p[0:64, :512], lhsT=w_ks[:, kc, g * 64:(g + 1) * 64], rhs=hT[:, kc, :],
                                                     start=(kc == 0), stop=(kc == 7)), reads=[w_ks, hT], writes=[p])
            k.op("act", lambda e: e.copy(out=Kaug[0:64, g, t0:t0 + 512], in_=p[0:64, :512]), reads=[p], writes=[Kaug])
        ph = k.ps()
        for j in range(4):
            for hc in range(2):
                for m in range(16):
                    k.op("pe", lambda e, j=j, hc=hc, m=m: e.matmul(ph[:, (j * 2 + hc) * 32:(j * 2 + hc) * 32 + 32], lhsT=w1[:, j // 2, m, hc * 128:(hc + 1) * 128],
                                                                     rhs=KCL[:, j, 2 * m + 1:2 * m + 498:16], start=(m == 0), stop=(m == 15)),
                         reads=[w1, KCL], writes=[ph])
        phv = ph[:, :256].rearrange("p (a c) -> p a c", a=8)
        k.op("dve", lambda e: e.tensor_tensor(out=gx[0][:], in0=phv, in1=PBf[:], op=ALU.add), reads=[ph, PBf], writes=[gx[0]])
        k.op("pool", lambda e: e.tensor_tensor(out=gx[1][:], in0=gx[0][:], in1=gx[0][:], op=ALU.mult), reads=[gx[0]], writes=[gx[1]])
        k.op("dve", lambda e: e.tensor_scalar(out=gx[1][:], in0=gx[1][:], scalar1=0.044715 * GELU_C, scalar2=GELU_C, op0=ALU.mult, op1=ALU.add), reads=[gx[1]], writes=[gx[1]])
        k.op("pool", lambda e: e.tensor_tensor(out=gx[1][:], in0=gx[1][:], in1=gx[0][:], op=ALU.mult), reads=[gx[1], gx[0]], writes=[gx[1]])
        k.op("act", lambda e: e.activation(out=gx[2][:], in_=gx[1][:], func=AF.Sigmoid), reads=[gx[1]], writes=[gx[2]])
        k.op("dve", lambda e: e.tensor_tensor(out=GH[:, :, 32 * i:32 * i + 32], in0=gx[2][:], in1=gx[0][:], op=ALU.mult), reads=[gx[2], gx[0]], writes=[GH])
        k.op("pool", lambda e: e.tensor_copy(out=KCL[:, :, 0:17], in_=KCL[:, :, 512:529]), reads=[KCL], writes=[KCL])
        for s in range(4):
            sl = slice(s * 128, (s + 1) * 128)
            v = Vr[s % 2]
            for gi, (c0, c1) in enumerate(((0, 512), (512, 1024), (1024, 1280))):
                p = k.ps()
                for kc in range(8):
                    k.op("pe", lambda e, kc=kc: e.matmul(p[:, :c1 - c0], lhsT=hT[:, kc, sl], rhs=w_tm[:, kc, c0:c1], start=(kc == 0), stop=(kc == 7)),
                         reads=[hT, w_tm], writes=[p])
                if gi == 0:
                    k.op("act", lambda e: e.copy(out=v[:], in_=p[:, :512]), reads=[p], writes=[v])
                elif gi == 1:
                    k.op("act", lambda e: e.activation(out=SG[:], in_=p[:, :512], func=AF.Silu), reads=[p], writes=[SG])
                else:
                    n = 4 * i + s
                    k.op("dve", lambda e: e.tensor_copy(out=Vs[:, n, :, 0:64], in_=p[:, 0:128].rearrange("p (g d) -> p g d", g=2)), reads=[p], writes=[Vs])
                    k.op("dve", lambda e: e.tensor_copy(out=Vw[:, n, :, 0:64], in_=p[:, 128:256].rearrange("p (g d) -> p g d", g=2)), reads=[p], writes=[Vw])
            for _ in ret_chunk(i, s):
                pass
        k.store(co, YTs[i % 2], yT[:, 0:4, t0:t0 + 512], YTs[i % 2][:])

    for i in range(8):
        pass1_tile(i)
    for g in range(2):
        p = k.ps()
        for hc in range(2):
            k.op("pe", lambda e, hc=hc: e.matmul(p[0:64, :256], lhsT=w2[:, 0, hc, :], rhs=GH[:, g * 2 + hc, 0:256], start=(hc == 0), stop=(hc == 1)),
                 reads=[w2, GH], writes=[p])
        k.op("act", lambda e, g=g, p=p: e.copy(out=KCc[0:64, g, 0:256], in_=p[0:64, :256]), reads=[p], writes=[KCc])
        for ct in range(2):
            p2 = k.ps()
            for hc in range(2):
                k.op("pe", lambda e, hc=hc, p2=p2, ct=ct, g=g: e.matmul(p2[:, :64], lhsT=GH[:, (2 + g) * 2 + hc, 1 + 128 * ct:129 + 128 * ct], rhs=w2[:, 1, hc, :],
                                                                         start=(hc == 0), stop=(hc == 1)), reads=[w2, GH], writes=[p2])
            k.op("dve", lambda e, p2=p2, ct=ct, g=g: e.tensor_copy(out=VCaug[:, ct, g, 0:64], in_=p2[:, :64]), reads=[p2], writes=[VCaug])
    S.barrier()
    sc1.close()
    if dbg == "p1":
        return k

    sc2 = k.st
    k.mkpool(3, tag="b")
    OC = [k.psum("OC%d" % i, [128, 512], F32) for i in range(2)]
    O2 = k.psum("O2", [128, 512], F32)
    O3 = k.psum("O3", [128, 512], F32)
    w_q = k.sb("w_q", [128, 8, 1048], BF16)
    Qaug = k.sb("Qaug", [128, 8, 512], BF16)
    QW1 = k.sb("QW1", [128, 4, 512], BF16)
    GT = k.sb("GT", [128, 4, 24], F32)
    trib = k.sb("trib", [128, 2, 512], BF16)
    cmt = [k.sb("cmt%d" % i, [128, 2, 128], BF16) for i in range(2)]
    sct = [k.sb("sct%d" % i, [128, 3, 64], F32) for i in range(2)]
    Pb = [k.sb("Pb%d" % i, [128, 512], BF16) for i in range(4)]
    r1 = k.sb("r1", [128, 4], F32)
    imp = k.sb("imp", [128, 64], F32)
    imp2 = k.sb("imp2", [128, 64], F32)
    t8 = k.sb("t8", [128, 8], F32)
    t8b = k.sb("t8b", [128, 8], F32)
    msk = k.sb("msk", [128, 64], F32)
    BR = k.sb("BR", [128, 128], BF16)
    cf = [k.sb("cf%d" % i, [128, 4], F32) for i in range(3)]
    y1 = k.sb("y1", [128, 4, 64], F32)
    y2 = k.sb("y2", [128, 4, 64], F32)
    cq = S.channel("q")
    k.load(cw, w_q, w_q[:], w_q_d)
    k.load(cc, trib, trib[:], trib_d)
    k.op("pool", lambda e: e.memset(BR[:], 0.0), writes=[BR])
    pbi = [0]

    def nextP():
        p = Pb[pbi[0] % 4]
        pbi[0] += 1
        return p

    def v4(ap):
        return ap.rearrange("p (h q) -> p h q", h=4)

    def qtile(i, s):
        qt = 4 * i + s
        sl = slice(s * 128, (s + 1) * 128)
        cm_t = cmt[qt % 2]
        sc_t = sct[qt % 2]
        k.load(cq, cm_t, cm_t[:], cm_d[qt])
        k.load(cq, sc_t, sc_t[:], selc_d[qt])
        for g in range(2):
            hs = slice(4 * g, 4 * g + 4)
            nct = 1 if qt <= 15 else 2
            for ct in range(nct):
                p = k.ps()
                k.op("pe", lambda e, p=p, ct=ct: e.matmul(v4(p[:, :512]), lhsT=KCc[0:64, g, 1 + 128 * ct:129 + 128 * ct], rhs=Qaug[0:64, hs, sl], start=True, stop=True),
                     reads=[KCc, Qaug], writes=[p])
                P1 = nextP()
                k.op("act", lambda e, p=p, P1=P1: e.activation(out=P1[:], in_=p[:, :512], func=AF.Exp, scale=NSA_SCALE), reads=[p], writes=[P1])
                k.op("pool", lambda e, P1=P1, ct=ct: e.tensor_tensor(out=v4(P1[:]), in0=v4(P1[:]), in1=cm_t[:, ct, :].unsqueeze(1).broadcast_to([128, 4, 128]), op=ALU.mult),
                     reads=[P1, cm_t], writes=[P1])
                for h in range(4):
                    o = OC[h // 2]
                    k.op("pe", lambda e, P1=P1, ct=ct, h=h, o=o: e.matmul(o[:, (h % 2) * 129:(h % 2) * 129 + 129], lhsT=P1[:, h * 128:(h + 1) * 128], rhs=VCaug[:, ct, g, :],
                                                                           start=(ct == 0 and h % 2 == 0), stop=(ct == nct - 1 and h % 2 == 1)), reads=[P1, VCaug], writes=[o])
            ocv = [OC[b][:, 0:258].rearrange("p (h c) -> p h c", h=2) for b in range(2)]
            for b in range(2):
                k.op("dve", lambda e, b=b: e.tensor_scalar(out=r1[:, 2 * b:2 * b + 2], in0=ocv[b][:, :, 64], scalar1=1e-30, scalar2=None, op0=ALU.add), reads=[OC[b]], writes=[r1])
            k.op("dve", lambda e: e.reciprocal(out=r1[:], in_=r1[:]), reads=[r1], writes=[r1])
            for h in range(4):
                if h == 0:
                    k.op("dve", lambda e: e.tensor_scalar(out=imp[:], in0=ocv[0][:, 0, 65:129], scalar1=r1[:, 0:1], scalar2=None, op0=ALU.mult), reads=[OC[0], r1], writes=[imp])
                else:
                    k.op("dve", lambda e, h=h: e.scalar_tensor_tensor(out=imp[:], in0=ocv[h // 2][:, h % 2, 65:129], scalar=r1[:, h:h + 1], in1=imp[:], op0=ALU.mult, op1=ALU.add),
                         reads=[OC[h // 2], r1, imp], writes=[imp])
            k.op("dve", lambda e: e.tensor_tensor(out=imp[:], in0=imp[:], in1=sc_t[:, 0, :], op=ALU.max), reads=[imp, sc_t], writes=[imp])
            k.op("dve", lambda e: e.tensor_tensor(out=imp[:], in0=imp[:], in1=sc_t[:, 1, :], op=ALU.add), reads=[imp, sc_t], writes=[imp])
            k.op("dve", lambda e: e.max(out=t8[:], in_=imp[:]), reads=[imp], writes=[t8])
            k.op("dve", lambda e: e.match_replace(out=imp2[:], in_to_replace=t8[:], in_values=imp[:], imm_value=-3.0e38), reads=[imp, t8], writes=[imp2])
            k.op("dve", lambda e: e.max(out=t8b[:], in_=imp2[:]), reads=[imp2], writes=[t8b])
            k.op("dve", lambda e: e.tensor_scalar(out=msk[:], in0=imp[:], scalar1=t8b[:, 7:8], scalar2=None, op0=ALU.is_ge), reads=[imp, t8b], writes=[msk])
            k.op("dve", lambda e: e.tensor_tensor(out=msk[:], in0=msk[:], in1=sc_t[:, 2, :], op=ALU.mult), reads=[msk, sc_t], writes=[msk])
            k.op("dve", lambda e: e.tensor_scalar(out=BR[:, 64:128], in0=msk[:], scalar1=BIG, scalar2=-BIG, op0=ALU.mult, op1=ALU.add), reads=[msk], writes=[BR])
            k.op("pe", lambda e: e.transpose(TRP[:, 0:128], BR[:], ident[:]), reads=[BR, ident], writes=[TRP])
            k.op("act", lambda e: e.copy(out=Qaug[64:128, hs, sl], in_=TRP[64:128, 0:128].unsqueeze(1).broadcast_to([64, 4, 128])), reads=[TRP], writes=[Qaug])
            for kt in range(qt + 1):
                p = k.ps()
                k.op("pe", lambda e, p=p, kt=kt: e.matmul(v4(p[:, :512]), lhsT=Kaug[:, g, kt * 128:(kt + 1) * 128], rhs=Qaug[:, hs, sl], start=True, stop=(kt != qt)),
                     reads=[Kaug, Qaug], writes=[p])
                if kt == qt:
                    k.op("pe", lambda e, p=p: e.matmul(p[:, :512], lhsT=ident[:], rhs=trib[:, 0, :], start=False, stop=True), reads=[ident, trib], writes=[p])
                P2 = nextP()
                k.op("act", lambda e, p=p, P2=P2: e.activation(out=P2[:], in_=p[:, :512], func=AF.Exp, scale=NSA_SCALE), reads=[p], writes=[P2])
                for h in range(4):
                    k.op("pe", lambda e, P2=P2, kt=kt, h=h: e.matmul(O2[:, h * 65:(h + 1) * 65], lhsT=P2[:, h * 128:(h + 1) * 128], rhs=Vs[:, kt, g, :],
                                                                      start=(kt == 0 and h == 0), stop=(kt == qt and h == 3)), reads=[P2, Vs], writes=[O2])
            k0 = max(0, qt - 4)
            for kt in range(k0, qt + 1):
                p = k.ps()
                edge = (kt == qt) or (kt == qt - 4)
                if g == 0:
                    k.op("pe", lambda e, p=p, kt=kt, edge=edge: e.matmul(v4(p[:, :512]), lhsT=KWc[0:64, kt * 128:(kt + 1) * 128], rhs=Qaug[0:64, 0:4, sl], start=True, stop=(not edge)),
                         reads=[KWc, Qaug], writes=[p])
                else:
                    k.op("pe", lambda e, p=p, kt=kt, edge=edge: e.matmul(v4(p[:, :512]), lhsT=KWc[64:128, kt * 128:(kt + 1) * 128], rhs=QW1[64:128, 0:4, sl], start=True, stop=(not edge)),
                         reads=[KWc, QW1], writes=[p])
                if edge:
                    ti = 0 if kt == qt else 1
                    k.op("pe", lambda e, p=p, ti=ti: e.matmul(p[:, :512], lhsT=ident[:], rhs=trib[:, ti, :], start=False, stop=True), reads=[ident, trib], writes=[p])
                P3 = nextP()
                k.op("act", lambda e, p=p, P3=P3: e.activation(out=P3[:], in_=p[:, :512], func=AF.Exp, scale=NSA_SCALE), reads=[p], writes=[P3])
                for h in range(4):
                    k.op("pe", lambda e, P3=P3, kt=kt, h=h: e.matmul(O3[:, h * 65:(h + 1) * 65], lhsT=P3[:, h * 128:(h + 1) * 128], rhs=Vw[:, kt, g, :],
                                                                      start=(kt == k0 and h == 0), stop=(kt == qt and h == 3)), reads=[P3, Vw], writes=[O3])
            o2v = O2[:, 0:260].rearrange("p (h c) -> p h c", h=4)
            o3v = O3[:, 0:260].rearrange("p (h c) -> p h c", h=4)
            gsl = lambda br: GT[:, s, 12 * g + br:12 * g + 12:3]
            k.op("dve", lambda e: e.tensor_tensor(out=cf[0][:], in0=r1[:], in1=gsl(0), op=ALU.mult), reads=[r1, GT], writes=[cf[0]])
            k.op("dve", lambda e: e.reciprocal(out=cf[1][:], in_=o2v[:, :, 64]), reads=[O2], writes=[cf[1]])
            k.op("dve", lambda e: e.tensor_tensor(out=cf[1][:], in0=cf[1][:], in1=gsl(1), op=ALU.mult), reads=[cf[1], GT], writes=[cf[1]])
            k.op("dve", lambda e: e.reciprocal(out=cf[2][:], in_=o3v[:, :, 64]), reads=[O3], writes=[cf[2]])
            k.op("dve", lambda e: e.tensor_tensor(out=cf[2][:], in0=cf[2][:], in1=gsl(2), op=ALU.mult), reads=[cf[2], GT], writes=[cf[2]])
            bc = lambda t, a, b: t[:, a:b].unsqueeze(2).broadcast_to([128, b - a, 64])
            for b in range(2):
                k.op("dve", lambda e, b=b: e.tensor_tensor(out=y1[:, 2 * b:2 * b + 2, :], in0=ocv[b][:, :, 0:64], in1=bc(cf[0], 2 * b, 2 * b + 2), op=ALU.mult),
                     reads=[OC[b], cf[0]], writes=[y1])
            k.op("dve", lambda e: e.tensor_tensor(out=y2[:], in0=o2v[:, :, 0:64], in1=bc(cf[1], 0, 4), op=ALU.mult), reads=[O2, cf[1]], writes=[y2])
            k.op("pool", lambda e: e.tensor_tensor(out=y1[:], in0=y1[:], in1=y2[:], op=ALU.add), reads=[y1, y2], writes=[y1])
            k.op("dve", lambda e: e.tensor_tensor(out=y2[:], in0=o3v[:, :, 0:64], in1=bc(cf[2], 0, 4), op=ALU.mult), reads=[O3, cf[2]], writes=[y2])
            k.op("pool", lambda e: e.tensor_tensor(out=ybf[:, 256 * g:256 * g + 256].rearrange("p (h d) -> p h d", h=4), in0=y1[:], in1=y2[:], op=ALU.add),
                 reads=[y1, y2], writes=[ybf])
        transposes_out(ybf, YTs[i % 2], s)

    def pass2_tile(i):
        t0 = 512 * i
        load_norm(t0)
        for h in range(8):
            p = k.ps()
            M = 64 if h < 4 else 128
            for kc in range(8):
                k.op("pe", lambda e, kc=kc, p=p, h=h, M=M: e.matmul(p[0:M, :512], lhsT=w_q[:, kc, h * 128:h * 128 + M], rhs=hT[:, kc, :], start=(kc == 0), stop=(kc == 7)),
                     reads=[w_q, hT], writes=[p])
            k.op("act", lambda e, p=p, h=h: e.copy(out=Qaug[0:64, h, :], in_=p[0:64, :512]), reads=[p], writes=[Qaug])
            if h >= 4:
                k.op("dve", lambda e, p=p, h=h: e.tensor_copy(out=QW1[64:128, h - 4, :], in_=p[64:128, :512]), reads=[p], writes=[QW1])
        for s in range(4):
            p = k.ps()
            for kc in range(8):
                k.op("pe", lambda e, kc=kc, p=p, s=s: e.matmul(p[:, :24], lhsT=hT[:, kc, s * 128:(s + 1) * 128], rhs=w_q[:, kc, 1024:1048], start=(kc == 0), stop=(kc == 7)),
                     reads=[w_q, hT], writes=[p])
            k.op("act", lambda e, p=p, s=s: e.activation(out=GT[:, s, :], in_=p[:, :24], func=AF.Sigmoid), reads=[p], writes=[GT])
        for s in range(4):
            qtile(i, s)
        k.store(co, YTs[i % 2], yT[:, 4:8, t0:t0 + 512], YTs[i % 2][:])

    for i in range(8):
        pass2_tile(i)
    return k


def A_consts():
    if "Ac" in _CACHE:
        return _CACHE["Ac"]
    out = []
    pos = np.arange(SEQ, dtype=np.float32)
    theta = (np.float32(10000.0) ** (-np.linspace(0.0, 1.0, 64, dtype=np.float32))).astype(np.float32)
    ang = (pos[:, None] * theta[None, :]).astype(np.float32)
    cos = np.cos(ang).astype(np.float32).T
    sin = np.sin(ang).astype(np.float32).T
    cs = np.zeros((128, 2, SEQ), np.float32)
    cs[:64, 0] = cos
    cs[64:, 0] = cos
    cs[:64, 1] = -sin
    cs[64:, 1] = sin
    j = np.arange(128, dtype=np.float32)
    kaE = np.zeros((64, SEQ), NPBF)
    for jj in range(64):
        kaE[jj, jj * 64:(jj + 1) * 64] = 1
    kk = np.arange(128)[:, None]
    qq = np.arange(128)[None, :]
    trib = np.zeros((128, 2, 4, 128), np.float32)
    trib[:, 0] = np.where(kk <= qq, 0.0, -BIG)[:, None, :]
    trib[:, 1] = np.where(kk > qq, 0.0, -BIG)[:, None, :]
    trib = trib.reshape(128, 2, 512).astype(NPBF)
    ident = np.eye(128, dtype=np.float32).astype(NPBF)
    cm = np.zeros((32, 128, 2, 128), np.float32)
    selc = np.zeros((32, 128, 3, 64), np.float32)
    for qt in range(32):
        t = 128 * qt + np.arange(128)
        for ct in range(2):
            c = 128 * ct + np.arange(128)
            cm[qt, :, ct, :] = ((16 * c[:, None] + 31 <= t[None, :]) & (c[:, None] <= 254)).astype(np.float32)
        jb = np.arange(64)[None, :]
        cur = (t // 64)[:, None]
        causal = (jb * 64 <= t[:, None])
        forced = ((jb == 0) | (jb == cur) | (jb == cur - 1)) & causal
        selc[qt, :, 0, :] = np.where(forced, 1e6, 0.0)
        selc[qt, :, 1, :] = np.where(causal, 0.0, -1e30)
        selc[qt, :, 2, :] = causal.astype(np.float32)
    ovl = np.zeros((128, 2, 65), np.float32)
    for ct in range(2):
        c = 128 * ct + np.arange(128)
        ci = 16 * c[:, None]
        sj = 64 * np.arange(64)[None, :]
        ov = ((ci < sj + 64) & (ci + 32 > sj) & (c[:, None] <= 254)).astype(np.float32)
        ovl[:, ct, 0] = (c <= 254).astype(np.float32)
        ovl[:, ct, 1:] = ov
    for hh in range(2):
        rc = np.zeros((128, 2, 3, 128), np.float32)
        gd = np.zeros((128, 2), np.float32)
        for hl in range(2):
            H = 2 * hh + hl
            lg = np.log1p(-(2.0 ** (-5.0 - H)))
            rel = j[None, :] - j[:, None]
            dm = np.where(rel >= 0, np.exp(lg * np.maximum(rel, 0.0)), 0.0) * (128.0 ** -0.5)
            rc[:, hl, 0, :] = dm
            rc[:, hl, 1, :] = np.exp(lg * (j + 1.0))[None, :]
            rc[:, hl, 2, :] = (np.exp(lg * (127.0 - j)) * (128.0 ** -0.5))[None, :]
            gd[:, hl] = np.exp(lg * 128.0)
        out.append(dict(cs=cs, rc=rc.astype(np.float32), gd=gd.astype(np.float32), kaE=kaE, trib=trib, ident=ident,
                        cm=cm.astype(NPBF), selc=selc, ovl=ovl.astype(NPBF)))
    _CACHE["Ac"] = out
    return out


def A_weights(W, inputs, l, hh):
    w_in = W["w_in"][l]
    cols = []

    def swp(a):
        return np.concatenate([a[:, 64:128], a[:, 0:64]], axis=1)
    for nm in ("rq", "rk"):
        for hl in range(2):
            H = 2 * hh + hl
            a = w_in[:, IN_OFF[nm] + 128 * H: IN_OFF[nm] + 128 * H + 128]
            cols += [a, swp(a)]

    def nkv(ty, G):
        o = IN_OFF["nkv"] + (ty * 4 + G) * 64
        return w_in[:, o:o + 64]
    for ty in (0, 1):
        for gl in range(2):
            a = nkv(ty, 2 * hh + gl)
            cols += [a, a]
    cols += [nkv(4, 2 * hh), nkv(4, 2 * hh + 1)]
    w_fm = fm(np.concatenate(cols, axis=1))
    w_ks = fm(np.concatenate([nkv(2, 2 * hh), nkv(2, 2 * hh + 1)], axis=1))
    w_tm = fm(np.concatenate([w_in[:, IN_OFF["rv"] + 512 * hh: IN_OFF["rv"] + 512 * hh + 512],
                              w_in[:, IN_OFF["rg"] + 512 * hh: IN_OFF["rg"] + 512 * hh + 512],
                              nkv(3, 2 * hh), nkv(3, 2 * hh + 1), nkv(5, 2 * hh), nkv(5, 2 * hh + 1)], axis=1))
    qc = []
    for h in range(8):
        Hq = 8 * hh + h
        a = w_in[:, IN_OFF["nq"] + 64 * Hq: IN_OFF["nq"] + 64 * Hq + 64]
        qc += [a, a]
    qc.append(w_in[:, IN_OFF["ng"] + 24 * hh: IN_OFF["ng"] + 24 * hh + 24])
    w_q = fm(np.concatenate(qc, axis=1))
    w1 = np.ascontiguousarray(W["nsa_cmp_w1"][l].reshape(2, 16, 128, 256).transpose(2, 0, 1, 3))
    w2 = np.ascontiguousarray(W["nsa_cmp_w2"][l].reshape(2, 2, 128, 64).transpose(2, 0, 1, 3))
    posb = np.ascontiguousarray(inputs["nsa_cmp_pos"][l].reshape(2, 16, 128).transpose(2, 0, 1)).astype(np.float32)
    nwA = np.ascontiguousarray(inputs["attn_norm_w"][l].reshape(8, 128).T)
    rnw = np.ascontiguousarray(np.broadcast_to(inputs["ret_norm_w"][l][512 * hh:512 * hh + 512][None, :], (128, 512))).astype(np.float32)
    return dict(w_fm=w_fm, w_ks=w_ks, w_tm=w_tm, w_q=w_q, w1=w1, w2=w2, posb=posb, nwA=nwA, rnw=rnw)


def run_A(W, inputs, l, xT_full, dbg=None):
    nc = _get("A%s" % dbg, lambda: build_A(dbg))
    AC = A_consts()
    maps = []
    for b in range(4):
        for hh in range(2):
            m = dict(AC[hh])
            m.update(A_weights(W, inputs, l, hh))
            m["xT"] = fm(xT_full[b])
            maps.append(m)
    res = run_bass_kernel_spmd(nc, maps, core_ids=list(range(8)))
    yr = np.zeros((4, 1024, SEQ), NPBF)
    yn = np.zeros((4, 1024, SEQ), NPBF)
    for b in range(4):
        for hh in range(2):
            r = np.asarray(res.results[b * 2 + hh]["yT"])
            yr[b, 512 * hh:512 * hh + 512] = r[:, 0:4].transpose(1, 0, 2).reshape(512, SEQ)
            yn[b, 512 * hh:512 * hh + 512] = r[:, 4:8].transpose(1, 0, 2).reshape(512, SEQ)
    return yr, yn


def kernel(**inputs):
    inputs = {k_: np.asarray(v) for k_, v in inputs.items()}
    W = prep_weights(inputs)
    xT = np.ascontiguousarray(inputs["x"].astype(np.float32).transpose(0, 2, 1))
    for l in range(2):
        yr, yn = run_A(W, inputs, l, xT)
        xT = run_B(W, inputs, l, xT, yr, yn, final=(l == 1))
    return np.ascontiguousarray(xT.transpose(0, 2, 1)).astype(np.float32)
```
